# Optimizing a Trainium2 kernel written in Bass

```python
import jax, jax.numpy as jnp
from jax import lax
import numpy as np

D_MODEL = 2048
BATCH = 2
SEQ = 4096
DEPTH = 4

D_A = D_MODEL // 2
D_B = D_MODEL // 2
HEAD_DIM = 128
N_HEADS_A = D_A // HEAD_DIM
N_HEADS_B = D_B // HEAD_DIM
D_IN_EVEN = 2 * D_A + 3 * D_B
CONV_A_WIDTH = 31
CONV_B_WIDTH = 3
POOL_WINDOWS = (2, 4, 8, 16)
N_POOL_GROUPS = len(POOL_WINDOWS)
POOL_GROUP_DIM = D_MODEL // N_POOL_GROUPS
D_FF = 4 * D_MODEL
N_EVEN = (DEPTH + 1) // 2
N_ODD = DEPTH // 2
RMS_EPS = 1e-6
LN_EPS = 1e-5

kernel_name = "hybrid_conformerconv_shortconv_multipool_trunk"


def rmsnorm(x, g):
    xf = x.astype(jnp.float32)
    y = xf * lax.rsqrt(jnp.mean(xf * xf, axis=-1, keepdims=True) + RMS_EPS)
    return (y * g.astype(jnp.float32)).astype(x.dtype)


def layernorm(x, g, b):
    xf = x.astype(jnp.float32)
    mu = jnp.mean(xf, axis=-1, keepdims=True)
    xc = xf - mu
    var = jnp.mean(xc * xc, axis=-1, keepdims=True)
    y = xc * lax.rsqrt(var + LN_EPS) * g.astype(jnp.float32) + b.astype(jnp.float32)
    return y.astype(x.dtype)


def causal_depthwise_conv(x, w):
    k_width, channels = w.shape
    return lax.conv_general_dilated(
        x, w[:, None, :].astype(x.dtype),
        window_strides=(1,),
        padding=[(k_width - 1, 0)],
        dimension_numbers=("NWC", "WIO", "NWC"),
        feature_group_count=channels)


def conv_mixers(h, w_in, conv_a_w, conv_a_b, ln_a_g, ln_a_b, conv_b_w, w_out):
    u = jnp.einsum("bsd,de->bse", h, w_in)
    a_val, a_gate, b_x, b_c, b_b = jnp.split(
        u, [D_A, 2 * D_A, 2 * D_A + D_B, 2 * D_A + 2 * D_B], axis=-1)
    a = a_val * jax.nn.sigmoid(a_gate)
    a = causal_depthwise_conv(a, conv_a_w) + conv_a_b
    a = jax.nn.silu(layernorm(a, ln_a_g, ln_a_b))
    bo = b_b * causal_depthwise_conv(b_c * b_x, conv_b_w)
    return jnp.einsum("bse,ed->bsd", jnp.concatenate([a, bo], axis=-1), w_out)


def pool_mixer(h, pool_w, pool_scale):
    bsz, seq, _ = h.shape
    hg = h.reshape(bsz, seq, N_POOL_GROUPS, POOL_GROUP_DIM)
    csum = jnp.cumsum(hg.astype(jnp.float32), axis=1)
    pos = jnp.arange(1, seq + 1, dtype=jnp.float32)
    means = []
    for g, win in enumerate(POOL_WINDOWS):
        cg = csum[:, :, g]
        lagged = jnp.pad(cg, ((0, 0), (win, 0), (0, 0)))[:, :seq]
        count = jnp.minimum(pos, float(win))[None, :, None]
        means.append((cg - lagged) / count)
    pooled = jnp.stack(means, axis=2).astype(h.dtype) - hg
    mixed = jnp.einsum("bsgc,gce->bsge", pooled, pool_w).reshape(bsz, seq, D_MODEL)
    return mixed * pool_scale


def sq_relu_mlp(h, w1, w2):
    z = jnp.einsum("bsd,df->bsf", h, w1)
    z = jnp.square(jax.nn.relu(z))
    return jnp.einsum("bsf,fd->bsd", z, w2)


def setup_inputs(seed: int = 0) -> dict:
    key = jax.random.key(seed)
    ks = jax.random.split(key, 16)
    f32 = jnp.float32
    nrm = lambda k, shape, scale: jax.random.normal(k, shape, f32) * scale
    return {
        "x": jax.random.normal(ks[0], (BATCH, SEQ, D_MODEL), f32),
        "norm_mix_g": 1.0 + nrm(ks[1], (DEPTH, D_MODEL), 0.05),
        "norm_mlp_g": 1.0 + nrm(ks[2], (DEPTH, D_MODEL), 0.05),
        "w_in_even": nrm(ks[3], (N_EVEN, D_MODEL, D_IN_EVEN), D_MODEL ** -0.5),
        "conv_a_w": nrm(ks[4], (N_EVEN, CONV_A_WIDTH, D_A), CONV_A_WIDTH ** -0.5),
        "conv_a_b": nrm(ks[5], (N_EVEN, D_A), 0.02),
        "ln_a_g": 1.0 + nrm(ks[6], (N_EVEN, D_A), 0.05),
        "ln_a_b": nrm(ks[7], (N_EVEN, D_A), 0.02),
        "conv_b_w": nrm(ks[8], (N_EVEN, CONV_B_WIDTH, D_B), CONV_B_WIDTH ** -0.5),
        "w_out_even": nrm(ks[9], (N_EVEN, D_A + D_B, D_MODEL), (D_A + D_B) ** -0.5),
        "pool_w": nrm(ks[10], (N_ODD, N_POOL_GROUPS, POOL_GROUP_DIM, POOL_GROUP_DIM), POOL_GROUP_DIM ** -0.5),
        "pool_scale": 1.0 + nrm(ks[11], (N_ODD, D_MODEL), 0.1),
        "mlp_w1": nrm(ks[12], (DEPTH, D_MODEL, D_FF), D_MODEL ** -0.5),
        "mlp_w2": nrm(ks[13], (DEPTH, D_FF, D_MODEL), 0.5 * D_FF ** -0.5),
        "final_g": 1.0 + nrm(ks[14], (D_MODEL,), 0.05),
    }


def reference(x, norm_mix_g, norm_mlp_g, w_in_even, conv_a_w, conv_a_b, ln_a_g, ln_a_b,
              conv_b_w, w_out_even, pool_w, pool_scale, mlp_w1, mlp_w2, final_g):
    for layer in range(DEPTH):
        i = layer // 2
        h = rmsnorm(x, norm_mix_g[layer])
        if layer % 2 == 0:
            x = x + conv_mixers(h, w_in_even[i], conv_a_w[i], conv_a_b[i], ln_a_g[i],
                                ln_a_b[i], conv_b_w[i], w_out_even[i])
        else:
            x = x + pool_mixer(h, pool_w[i], pool_scale[i])
        h = rmsnorm(x, norm_mlp_g[layer])
        x = x + sq_relu_mlp(h, mlp_w1[layer], mlp_w2[layer])
    return rmsnorm(x, final_g)
```

```python
from contextlib import ExitStack

import numpy as np
import concourse.bass as bass
import concourse.mybir as mybir
from concourse.bass_utils import run_bass_kernel_spmd

F32 = mybir.dt.float32
BF16 = mybir.dt.bfloat16
AF = mybir.ActivationFunctionType
ALU = mybir.AluOpType

D = 2048
KC = 16
SEQ = 4096
OWN = 1024
HALO = 128
T = OWN + HALO
NT = 384
NN = T // NT
WB = 256
NBUF = 3
DEPTH = 4
RMS_EPS = 1e-6
LN_EPS = 1e-5
POOL_WINDOWS = (2, 4, 8, 16)

C_GMIX = 0
C_GMLP = C_GMIX + 64
C_GFIN = C_GMLP + 64
C_CAW = C_GFIN + 16
C_CAB = C_CAW + 496
C_LNG = C_CAB + 16
C_LNB = C_LNG + 16
C_CBW = C_LNB + 16
C_PSC = C_CBW + 48
C_MASK = C_PSC + 32
C_INVC = C_MASK + 1
C_EPS_RMS = C_INVC + 64
C_EPS_LN = C_EPS_RMS + 1
NCST = C_EPS_LN + 1


class Buf:
    __slots__ = ("name", "w", "r")

    def __init__(self, name):
        self.name = name
        self.w = None
        self.r = []


class Eng:
    def __init__(self, name, sem):
        self.name = name
        self.sem = sem
        self.count = 0
        self.seen = {}
        self.ops = []


class Prog:
    def __init__(self, nc, stack):
        self.nc = nc
        self.stack = stack
        self.eng = {}
        for n in ("pe", "dve", "act", "pool", "sp"):
            self.eng[n] = Eng(n, stack.enter_context(nc.semaphore("s_" + n)))

    def new_sem(self, name):
        return [self.stack.enter_context(self.nc.semaphore(name)), 0]

    def _waits(self, e, reads, writes, extra=()):
        deps = list(extra)
        for b in reads:
            if b.w is not None:
                deps.append(b.w)
        for b in writes:
            if b.w is not None:
                deps.append(b.w)
            deps.extend(b.r)
        need = {}
        for (sem, val) in deps:
            if e.name == "pe" and sem is e.sem:
                continue
            k = id(sem)
            if e.seen.get(k, 0) >= val:
                continue
            if k not in need or need[k][1] < val:
                need[k] = (sem, val)
        out = []
        for k, (sem, val) in need.items():
            e.seen[k] = val
            out.append((sem, val))
        return out

    @staticmethod
    def _commit(tok, reads, writes):
        for b in writes:
            b.w = tok
            b.r = []
        for b in reads:
            b.r.append(tok)

    def op(self, eng, fn, reads=(), writes=()):
        e = self.eng[eng]
        waits = self._waits(e, reads, writes)
        e.count += 1
        tok = (e.sem, e.count)
        e.ops.append((waits, fn, (e.sem, 1)))
        self._commit(tok, reads, writes)
        return tok

    def group(self, eng, fns, reads=(), writes=()):
        e = self.eng[eng]
        waits = self._waits(e, reads, writes)
        e.count += 1
        tok = (e.sem, e.count)
        last = len(fns) - 1
        for i, fn in enumerate(fns):
            e.ops.append((waits if i == 0 else [], fn, (e.sem, 1) if i == last else None))
        self._commit(tok, reads, writes)
        return tok

    def dma(self, eng, fn, dsem, reads=(), writes=()):
        e = self.eng[eng]
        waits = self._waits(e, reads, writes)
        dsem[1] += 16
        tok = (dsem[0], dsem[1])
        e.ops.append((waits, fn, (dsem[0], 16)))
        self._commit(tok, reads, writes)
        return tok

    def wait_tok(self, eng, tok):
        e = self.eng[eng]
        waits = self._waits(e, (), (), extra=[tok])
        if waits:
            e.ops.append((waits, None, None))

    def replay(self):
        handles = {"pe": "tensor", "dve": "vector", "act": "scalar", "pool": "gpsimd", "sp": "sync"}
        with self.nc.Block() as block:
            for n, attr in handles.items():
                e = self.eng[n]

                def body(h, e=e):
                    for waits, fn, inc in e.ops:
                        for (sem, val) in waits:
                            h.wait_ge(sem, val)
                        if fn is not None:
                            ins = fn(h)
                            if inc is not None:
                                ins.then_inc(inc[0], inc[1])

                getattr(block, attr)(body)


def build_program():
    nc = bass.Bass("TRN2", target_bir_lowering=False)
    x_d = nc.dram_tensor("x", [T, D], F32, kind="ExternalInput").ap()
    cst_d = nc.dram_tensor("cst", [128, NCST], F32, kind="ExternalInput").ap()
    id_d = nc.dram_tensor("ident", [128, 128], F32, kind="ExternalInput").ap()
    win_d = nc.dram_tensor("w_in", [2, D, 5120], F32, kind="ExternalInput").ap()
    wout_d = nc.dram_tensor("w_out", [2, D, D], F32, kind="ExternalInput").ap()
    pw_d = nc.dram_tensor("pool_w", [2, 4, 512, 512], F32, kind="ExternalInput").ap()
    w1_d = nc.dram_tensor("w1", [DEPTH, D, 4 * D], F32, kind="ExternalInput").ap()
    w2_d = nc.dram_tensor("w2", [DEPTH, 4 * D, D], F32, kind="ExternalInput").ap()
    y_d = nc.dram_tensor("y", [OWN, D], F32, kind="ExternalOutput").ap()

    with ExitStack() as st:
        P = Prog(nc, st)

        def sb(name, shape, dt):
            return st.enter_context(nc.sbuf_tensor(name, shape, dt))

        xs = sb("xs", [128, KC, T], F32)
        hb = sb("hb", [128, KC, T], BF16)
        cb = sb("cb", [128, KC, T], BF16)
        wblk = [sb(f"wblk{i}", [128, KC * WB], BF16) for i in range(NBUF)]
        cst = sb("cst_sb", [128, NCST], F32)
        ident = sb("ident_sb", [128, 128], F32)
        ones = sb("ones", [128, 128], BF16)
        TP = 32 + T
        scr = sb("scr", [128, 3 * TP + 3 * T], F32)
        tA = scr[:, 0:TP]
        tB = scr[:, TP:2 * TP]
        tC = scr[:, 2 * TP:3 * TP]
        rstd = scr[:, 3 * TP:3 * TP + T]
        mu = scr[:, 3 * TP + T:3 * TP + 2 * T]
        tD = scr[:, 3 * TP + 2 * T:3 * TP + 3 * T]
        xin = [scr[:, 0:D], scr[:, 3 * TP + T:3 * TP + T + D]]
        sq = [sb(f"sq{i}", [128, NT], BF16) for i in range(3)]
        rtall = sb("rtall", [128, 3 * NT], F32)
        rt = [rtall[:, i * NT:(i + 1) * NT] for i in range(3)]
        tP = rtall[:, 0:T]
        ps = st.enter_context(nc.psum_tensor("ps", [128, 8, 512], F32))

        G = 16
        NG = T // G

        def blist(name):
            return [Buf(f"{name}_{g}") for g in range(NG)]

        def R(bl, lo, hi):
            return bl[lo // G:(hi + G - 1) // G]

        X = [blist(f"x{c}") for c in range(KC)]
        H = [blist(f"h{c}") for c in range(KC)]
        C = [blist(f"c{c}") for c in range(KC)]
        Wb = [Buf(f"w{i}") for i in range(NBUF)]
        PS = [Buf(f"ps{i}") for i in range(8)]
        B_cst, B_id, B_ones = Buf("cst"), Buf("ident"), Buf("ones")
        B_tA, B_tB = Buf("tA"), Buf("tB")
        B_tC = blist("tC")
        B_rstd = blist("rstd")
        B_mu = blist("mu")
        B_tD = blist("tD")
        B_xin = [[B_tA, B_tB], B_mu + B_tD]
        B_sq = [Buf(f"sq{i}") for i in range(3)]
        B_rt = [Buf(f"rt{i}") for i in range(3)]

        ld_cst = P.new_sem("ld_cst")
        ld_id = P.new_sem("ld_id")
        xld = [P.new_sem("xld0"), P.new_sem("xld1")]
        wsem = [P.new_sem(f"wl{i}") for i in range(NBUF)]
        stsem = P.new_sem("st")

        cnt = {"main": 0, "aux": 0, "w": 0, "sq": 0, "rt": 0}

        def main_bank():
            b = cnt["main"] % 6
            cnt["main"] += 1
            return b

        def aux_bank():
            b = 6 + cnt["aux"] % 2
            cnt["aux"] += 1
            return b

        def split(t0):
            L = T - t0
            base = (L // 48) * 16
            extra = (L - 3 * base) // 16
            tiles = []
            lo = t0
            for i in range(3):
                w = base + (16 if i < extra else 0)
                tiles.append((lo, lo + w))
                lo += w
            assert lo == T and all(hi - lo_ <= NT for lo_, hi in tiles)
            return tiles

        def col(i):
            return cst[:, i:i + 1]

        P.dma("sp", lambda h: h.dma_start(out=cst[:], in_=cst_d), ld_cst, writes=[B_cst])
        P.dma("sp", lambda h: h.dma_start(out=ident[:], in_=id_d), ld_id, writes=[B_id])
        P.op("dve", lambda h: h.memset(ones[:], 1.0), writes=[B_ones])
        P.op("dve", lambda h: h.memset(tA[:], 0.0), writes=[B_tA])
        P.op("dve", lambda h: h.memset(tB[:], 0.0), writes=[B_tB])
        P.op("dve", lambda h: h.memset(tC[:], 0.0), writes=B_tC)

        for tt in range(T // 128):
            s = tt % 2
            P.dma("sp", lambda h, tt=tt, s=s: h.dma_start(out=xin[s][:], in_=x_d[tt * 128:(tt + 1) * 128, :]),
                  xld[s], writes=B_xin[s])
            for q in range(4):
                bk = main_bank()
                fns = [(lambda h, bk=bk, j=j, q=q, s=s: h.transpose(
                    ps[:, bk, j * 128:(j + 1) * 128], xin[s][:, (4 * q + j) * 128:(4 * q + j + 1) * 128], ident[:]))
                    for j in range(4)]
                P.group("pe", fns, reads=B_xin[s] + [B_id], writes=[PS[bk]])
                wr = []
                for j in range(4):
                    wr += R(X[4 * q + j], tt * 128, (tt + 1) * 128)
                if q % 2 == 0:
                    fn = lambda h, bk=bk, q=q, tt=tt: h.tensor_copy(
                        out=xs[:, 4 * q:4 * q + 4, tt * 128:(tt + 1) * 128],
                        in_=ps[:, bk, :].rearrange("p (j t) -> p j t", t=128))
                    P.op("dve", fn, reads=[PS[bk]], writes=wr)
                else:
                    fn = lambda h, bk=bk, q=q, tt=tt: h.activation(
                        out=xs[:, 4 * q:4 * q + 4, tt * 128:(tt + 1) * 128],
                        in_=ps[:, bk, :].rearrange("p (j t) -> p j t", t=128), func=AF.Copy)
                    P.op("act", fn, reads=[PS[bk]], writes=wr)

        def load_wblock(src_ap, nk, ncols):
            s = cnt["w"] % NBUF
            cnt["w"] += 1
            slot_gen[s] += 1
            assert ncols * 2 >= 512, "cast-DMA SBUF-side runs below 512 B faulted the device under trace (R3-R5)"
            view = wblk[s][:, 0:nk * ncols].rearrange("p (k m) -> p k m", m=ncols)
            P.dma("pool", lambda h: h.dma_start(out=view, in_=src_ap.rearrange("(k p) m -> p k m", p=128)),
                  wsem[s], writes=[Wb[s]])
            return (s, slot_gen[s]), view

        def mm_item(bk, s, view, mi, nk, src, chunks, lo, hi, SRC):
            fns = [(lambda h, k=k: h.matmul(ps[:, bk, 0:hi - lo], lhsT=view[:, k, mi * 128:(mi + 1) * 128],
                                            rhs=src[:, chunks[k], lo:hi], start=(k == 0), stop=(k == nk - 1)))
                   for k in range(nk)]
            s, gen = s
            assert slot_gen[s] == gen, f"weight slot {s} was reloaded (gen {slot_gen[s]}) while a gen-{gen} reader is still being emitted"
            rb = [Wb[s]]
            for k in range(nk):
                rb += R(SRC[chunks[k]], lo, hi)
            P.group("pe", fns, reads=rb, writes=[PS[bk]])

        ALLK = list(range(KC))
        slot_gen = [0] * NBUF

        def rms_rstd(tiles):
            for (lo, hi) in tiles:
                w = hi - lo
                bk = aux_bank()
                for c in range(KC):
                    i = cnt["sq"] % 3
                    cnt["sq"] += 1
                    P.op("act", lambda h, c=c, i=i, lo=lo, hi=hi, w=w: h.activation(out=sq[i][:, 0:w], in_=xs[:, c, lo:hi], func=AF.Square),
                         reads=R(X[c], lo, hi), writes=[B_sq[i]])
                    P.op("pe", lambda h, c=c, i=i, bk=bk, w=w: h.matmul(ps[:, bk, 0:w], lhsT=ones[:], rhs=sq[i][:, 0:w],
                                                                     start=(c == 0), stop=(c == KC - 1)),
                         reads=[B_sq[i], B_ones], writes=[PS[bk]])
                P.op("act", lambda h, lo=lo, hi=hi, w=w, bk=bk: h.activation(out=tD[:, lo:hi], in_=ps[:, bk, 0:w], func=AF.Sqrt,
                                                                        bias=col(C_EPS_RMS), scale=1.0 / D),
                     reads=[PS[bk], B_cst], writes=R(B_tD, lo, hi))
                P.op("dve", lambda h, lo=lo, hi=hi: h.reciprocal(out=rstd[:, lo:hi], in_=tD[:, lo:hi]),
                     reads=R(B_tD, lo, hi), writes=R(B_rstd, lo, hi))

        def rmsnorm_to_hb(gbase, tiles):
            rms_rstd(tiles)
            for (lo, hi) in tiles:
                for c in range(KC):
                    P.op("dve", lambda h, c=c, lo=lo, hi=hi: h.scalar_tensor_tensor(
                        out=hb[:, c, lo:hi], in0=xs[:, c, lo:hi], scalar=col(gbase + c), in1=rstd[:, lo:hi],
                        op0=ALU.mult, op1=ALU.mult),
                        reads=R(X[c], lo, hi) + R(B_rstd, lo, hi) + [B_cst], writes=R(H[c], lo, hi))

        def halo_mask():
            xb = []
            for c in range(KC):
                xb += R(X[c], 0, HALO)
            P.op("dve", lambda h: h.tensor_scalar(out=xs[:, :, 0:HALO], in0=xs[:, :, 0:HALO], scalar1=col(C_MASK),
                                                  scalar2=None, op0=ALU.mult),
                 reads=[B_cst] + xb, writes=xb)


        def even_mixer(layer, utiles, otiles):
            i = layer // 2
            u0 = utiles[0][0]
            o0 = otiles[0][0]
            assert o0 - u0 >= 30
            rmsnorm_to_hb(C_GMIX + layer * KC, utiles)
            wv = {}

            def A_mm(j):
                blk, mi = divmod(j, 2)
                if mi == 0:
                    wv["g"] = load_wblock(win_d[i][:, 1024 + blk * WB:1024 + (blk + 1) * WB], KC, WB)
                    wv["v"] = load_wblock(win_d[i][:, blk * WB:(blk + 1) * WB], KC, WB)
                sg_, vg = wv["g"]
                sv_, vv = wv["v"]
                for (lo, hi) in utiles:
                    bk = main_bank()
                    mm_item(bk, sg_, vg, mi, KC, hb, ALLK, lo, hi, H)
                    P.op("act", lambda h, bk=bk, lo=lo, hi=hi: h.activation(out=tC[:, lo:hi], in_=ps[:, bk, 0:hi - lo], func=AF.Sigmoid),
                         reads=[PS[bk]], writes=R(B_tC, lo, hi))
                for (lo, hi) in utiles:
                    bk = main_bank()
                    mm_item(bk, sv_, vv, mi, KC, hb, ALLK, lo, hi, H)
                    P.op("dve", lambda h, bk=bk, lo=lo, hi=hi: h.tensor_tensor(
                        out=tA[:, 30 + lo:30 + hi], in0=ps[:, bk, 0:hi - lo], in1=tC[:, lo:hi], op=ALU.mult),
                        reads=[PS[bk]] + R(B_tC, lo, hi), writes=[B_tA])

            def A_conv(j):
                wbase = C_CAW + (i * 8 + j) * 31
                P.op("dve", lambda h: h.tensor_scalar(out=tB[:, o0:T], in0=tA[:, o0:T], scalar1=col(wbase),
                                                      scalar2=col(C_CAB + i * 8 + j), op0=ALU.mult, op1=ALU.add),
                     reads=[B_tA, B_cst], writes=[B_tB])
                for k in range(1, 30):
                    P.op("dve", lambda h, k=k: h.scalar_tensor_tensor(
                        out=tB[:, o0:T], in0=tA[:, o0 + k:T + k], scalar=col(wbase + k), in1=tB[:, o0:T],
                        op0=ALU.mult, op1=ALU.add),
                        reads=[B_tA, B_tB, B_cst], writes=[B_tB])
                P.op("dve", lambda h: h.scalar_tensor_tensor(
                    out=cb[:, j, o0:T], in0=tA[:, o0 + 30:T + 30], scalar=col(wbase + 30), in1=tB[:, o0:T],
                    op0=ALU.mult, op1=ALU.add),
                    reads=[B_tA, B_tB, B_cst], writes=R(C[j], o0, T))

            def B_mm(j):
                blk, mi = divmod(j, 2)
                if mi == 0:
                    wv["x"] = load_wblock(win_d[i][:, 2048 + blk * WB:2048 + (blk + 1) * WB], KC, WB)
                    wv["c"] = load_wblock(win_d[i][:, 3072 + blk * WB:3072 + (blk + 1) * WB], KC, WB)
                    wv["b"] = load_wblock(win_d[i][:, 4096 + blk * WB:4096 + (blk + 1) * WB], KC, WB)
                sx_, vx = wv["x"]
                sc_, vc = wv["c"]
                sb_, vb = wv["b"]
                for (lo, hi) in utiles:
                    bk = main_bank()
                    mm_item(bk, sx_, vx, mi, KC, hb, ALLK, lo, hi, H)
                    P.op("act", lambda h, bk=bk, lo=lo, hi=hi: h.activation(out=mu[:, lo:hi], in_=ps[:, bk, 0:hi - lo], func=AF.Copy),
                         reads=[PS[bk]], writes=R(B_mu, lo, hi))
                for (lo, hi) in utiles:
                    bk = main_bank()
                    mm_item(bk, sc_, vc, mi, KC, hb, ALLK, lo, hi, H)
                    P.op("dve", lambda h, bk=bk, lo=lo, hi=hi: h.tensor_tensor(
                        out=tD[:, lo:hi], in0=ps[:, bk, 0:hi - lo], in1=mu[:, lo:hi], op=ALU.mult),
                        reads=[PS[bk]] + R(B_mu, lo, hi), writes=R(B_tD, lo, hi))
                wbase = C_CBW + (i * 8 + j) * 3
                P.op("dve", lambda h: h.tensor_scalar(out=rstd[:, o0:T], in0=tD[:, o0 - 2:T - 2], scalar1=col(wbase),
                                                      scalar2=None, op0=ALU.mult),
                     reads=R(B_tD, o0 - 2, T) + [B_cst], writes=R(B_rstd, o0, T))
                for k in range(1, 3):
                    P.op("dve", lambda h, k=k: h.scalar_tensor_tensor(
                        out=rstd[:, o0:T], in0=tD[:, o0 - 2 + k:T - 2 + k], scalar=col(wbase + k), in1=rstd[:, o0:T],
                        op0=ALU.mult, op1=ALU.add),
                        reads=R(B_tD, o0 - 2, T) + R(B_rstd, o0, T) + [B_cst], writes=R(B_rstd, o0, T))
                for (lo, hi) in otiles:
                    bk = main_bank()
                    mm_item(bk, sb_, vb, mi, KC, hb, ALLK, lo, hi, H)
                    P.op("dve", lambda h, bk=bk, lo=lo, hi=hi: h.tensor_tensor(
                        out=cb[:, 8 + j, lo:hi], in0=ps[:, bk, 0:hi - lo], in1=rstd[:, lo:hi], op=ALU.mult),
                        reads=[PS[bk]] + R(B_rstd, lo, hi), writes=R(C[8 + j], lo, hi))

            for r in range(4):
                A_mm(2 * r)
                A_conv(2 * r)
                A_mm(2 * r + 1)
                if r > 0:
                    B_mm(2 * r - 2)
                    B_mm(2 * r - 1)
                A_conv(2 * r + 1)
            B_mm(6)
            B_mm(7)
            for (lo, hi) in otiles:
                w = hi - lo
                b1 = aux_bank()
                b2 = aux_bank()
                for j in range(8):
                    q = cnt["sq"] % 3
                    cnt["sq"] += 1
                    P.op("act", lambda h, j=j, q=q, lo=lo, hi=hi, w=w: h.activation(out=sq[q][:, 0:w], in_=cb[:, j, lo:hi], func=AF.Square),
                         reads=R(C[j], lo, hi), writes=[B_sq[q]])
                    P.op("pe", lambda h, j=j, b1=b1, lo=lo, hi=hi, w=w: h.matmul(ps[:, b1, 0:w], lhsT=ones[:], rhs=cb[:, j, lo:hi],
                                                                               start=(j == 0), stop=(j == 7)),
                         reads=R(C[j], lo, hi) + [B_ones], writes=[PS[b1]])
                    P.op("pe", lambda h, j=j, q=q, b2=b2, w=w: h.matmul(ps[:, b2, 0:w], lhsT=ones[:], rhs=sq[q][:, 0:w],
                                                                     start=(j == 0), stop=(j == 7)),
                         reads=[B_sq[q], B_ones], writes=[PS[b2]])
                P.op("dve", lambda h, b1=b1, lo=lo, hi=hi, w=w: h.tensor_scalar(out=tA[:, lo:hi], in0=ps[:, b1, 0:w], scalar1=1.0 / 1024,
                                                                           scalar2=None, op0=ALU.mult),
                     reads=[PS[b1]], writes=[B_tA])
                P.op("dve", lambda h, lo=lo, hi=hi: h.tensor_tensor(out=tP[:, lo:hi], in0=tA[:, lo:hi], in1=tA[:, lo:hi], op=ALU.mult),
                     reads=[B_tA], writes=B_rt)
                P.op("dve", lambda h, b2=b2, lo=lo, hi=hi, w=w: h.scalar_tensor_tensor(
                    out=tP[:, lo:hi], in0=ps[:, b2, 0:w], scalar=1.0 / 1024, in1=tP[:, lo:hi],
                    op0=ALU.mult, op1=ALU.subtract),
                    reads=[PS[b2]] + B_rt, writes=B_rt)
                P.op("act", lambda h, lo=lo, hi=hi: h.activation(out=tP[:, lo:hi], in_=tP[:, lo:hi], func=AF.Sqrt,
                                                               bias=col(C_EPS_LN), scale=1.0),
                     reads=B_rt + [B_cst], writes=B_rt)
                P.op("dve", lambda h, lo=lo, hi=hi: h.reciprocal(out=tC[:, lo:hi], in_=tP[:, lo:hi]),
                     reads=B_rt, writes=R(B_tC, lo, hi))
            for j in range(8):
                P.op("dve", lambda h, j=j: h.tensor_tensor(out=tB[:, o0:T], in0=cb[:, j, o0:T], in1=tA[:, o0:T], op=ALU.subtract),
                     reads=R(C[j], o0, T) + [B_tA], writes=[B_tB])
                P.op("dve", lambda h: h.tensor_tensor(out=tB[:, o0:T], in0=tB[:, o0:T], in1=tC[:, o0:T], op=ALU.mult),
                     reads=[B_tB] + R(B_tC, o0, T), writes=[B_tB])
                P.op("act", lambda h, j=j: h.activation(out=cb[:, j, o0:T], in_=tB[:, o0:T], func=AF.Silu,
                                                      bias=col(C_LNB + i * 8 + j), scale=col(C_LNG + i * 8 + j)),
                     reads=[B_tB, B_cst], writes=R(C[j], o0, T))
            for blk in range(D // WB):
                so_, vo = load_wblock(wout_d[i][:, blk * WB:(blk + 1) * WB], KC, WB)
                for mi in range(2):
                    m = 2 * blk + mi
                    for (lo, hi) in otiles:
                        bk = main_bank()
                        mm_item(bk, so_, vo, mi, KC, cb, ALLK, lo, hi, C)
                        P.op("dve", lambda h, bk=bk, lo=lo, hi=hi, m=m: h.tensor_tensor(
                            out=xs[:, m, lo:hi], in0=ps[:, bk, 0:hi - lo], in1=xs[:, m, lo:hi], op=ALU.add),
                            reads=[PS[bk]] + R(X[m], lo, hi), writes=R(X[m], lo, hi))
            halo_mask()

        def odd_mixer(layer, stiles, tiles):
            i = layer // 2
            e0 = stiles[0][0]
            assert tiles[0][0] - e0 >= 15
            rms_rstd(stiles)
            PADW = 16
            for c in range(KC):
                g = c // 4
                win = POOL_WINDOWS[g]
                P.op("dve", lambda h, c=c: h.scalar_tensor_tensor(
                    out=tA[:, PADW + e0:PADW + T], in0=xs[:, c, e0:T], scalar=col(C_GMIX + layer * KC + c), in1=rstd[:, e0:T],
                    op0=ALU.mult, op1=ALU.mult),
                    reads=R(X[c], e0, T) + R(B_rstd, e0, T) + [B_cst], writes=[B_tA])
                src, sbuf_ = tA, [B_tA]
                dsts = [(tB, [B_tB]), (tC, B_tC)]
                sh = 1
                di = 0
                while sh < win:
                    dst, dbuf = dsts[di % 2]
                    di += 1
                    P.op("dve", lambda h, src=src, dst=dst, sh=sh: h.tensor_tensor(
                        out=dst[:, PADW + e0:PADW + T], in0=src[:, PADW + e0:PADW + T], in1=src[:, PADW + e0 - sh:PADW + T - sh], op=ALU.add),
                        reads=sbuf_, writes=dbuf)
                    src, sbuf_ = dst, dbuf
                    sh *= 2
                P.op("dve", lambda h, c=c, src=src, win=win: h.scalar_tensor_tensor(
                    out=hb[:, c, e0:T], in0=src[:, PADW + e0:PADW + T], scalar=1.0 / win, in1=tA[:, PADW + e0:PADW + T],
                    op0=ALU.mult, op1=ALU.subtract),
                    reads=sbuf_ + [B_tA], writes=R(H[c], e0, T))
                q = cnt["rt"] % 3
                cnt["rt"] += 1
                f0 = PADW + HALO
                P.op("dve", lambda h, src=src, g=g, q=q, f0=f0: h.tensor_tensor(
                    out=rt[q][:, 0:16], in0=src[:, f0:f0 + 16], in1=cst[:, C_INVC + g * 16:C_INVC + (g + 1) * 16], op=ALU.mult),
                    reads=sbuf_ + [B_cst], writes=[B_rt[q]])
                P.op("dve", lambda h, c=c, q=q, f0=f0: h.tensor_tensor(
                    out=hb[:, c, HALO:HALO + 16], in0=rt[q][:, 0:16], in1=tA[:, f0:f0 + 16], op=ALU.subtract),
                    reads=[B_rt[q], B_tA], writes=R(H[c], HALO, HALO + 16))
            for g in range(4):
                s_, v = load_wblock(pw_d[i][g], 4, 512)
                for mi in range(4):
                    m = 4 * g + mi
                    for (lo, hi) in tiles:
                        bk = main_bank()
                        mm_item(bk, s_, v, mi, 4, hb, [4 * g + k for k in range(4)], lo, hi, H)
                        P.op("dve", lambda h, bk=bk, lo=lo, hi=hi, m=m: h.scalar_tensor_tensor(
                            out=xs[:, m, lo:hi], in0=ps[:, bk, 0:hi - lo], scalar=col(C_PSC + i * KC + m), in1=xs[:, m, lo:hi],
                            op0=ALU.mult, op1=ALU.add),
                            reads=[PS[bk], B_cst] + R(X[m], lo, hi), writes=R(X[m], lo, hi))
            halo_mask()

        def mlp(layer, tiles):
            rmsnorm_to_hb(C_GMLP + layer * KC, tiles)
            for grp in range(4):
                for blk in range(8):
                    c0 = grp * D + blk * WB
                    s_, v = load_wblock(w1_d[layer][:, c0:c0 + WB], KC, WB)
                    for mi in range(2):
                        f = 2 * blk + mi
                        for (lo, hi) in tiles:
                            w = hi - lo
                            bk = main_bank()
                            mm_item(bk, s_, v, mi, KC, hb, ALLK, lo, hi, H)
                            q = cnt["rt"] % 3
                            cnt["rt"] += 1
                            P.op("act", lambda h, bk=bk, q=q, w=w: h.activation(out=rt[q][:, 0:w], in_=ps[:, bk, 0:w], func=AF.Relu),
                                 reads=[PS[bk]], writes=[B_rt[q]])
                            P.op("dve", lambda h, q=q, f=f, lo=lo, hi=hi, w=w: h.tensor_tensor(
                                out=cb[:, f, lo:hi], in0=rt[q][:, 0:w], in1=rt[q][:, 0:w], op=ALU.mult),
                                reads=[B_rt[q]], writes=R(C[f], lo, hi))
                for blk in range(8):
                    s_, v = load_wblock(w2_d[layer][grp * D:(grp + 1) * D, blk * WB:(blk + 1) * WB], KC, WB)
                    for mi in range(2):
                        m = 2 * blk + mi
                        for (lo, hi) in tiles:
                            bk = main_bank()
                            mm_item(bk, s_, v, mi, KC, cb, ALLK, lo, hi, C)
                            P.op("dve", lambda h, bk=bk, lo=lo, hi=hi, m=m: h.tensor_tensor(
                                out=xs[:, m, lo:hi], in0=ps[:, bk, 0:hi - lo], in1=xs[:, m, lo:hi], op=ALU.add),
                                reads=[PS[bk]] + R(X[m], lo, hi), writes=R(X[m], lo, hi))

        even_mixer(0, split(32), split(64))
        mlp(0, split(64))
        odd_mixer(1, split(64), split(80))
        mlp(1, split(80))
        even_mixer(2, split(80), split(112))
        mlp(2, split(112))
        odd_mixer(3, split(112), split(128))
        mlp(3, split(128))

        ftiles = split(HALO)
        rms_rstd(ftiles)
        for (lo, hi) in ftiles:
            for c in range(KC):
                P.op("dve", lambda h, c=c, lo=lo, hi=hi: h.scalar_tensor_tensor(
                    out=xs[:, c, lo:hi], in0=xs[:, c, lo:hi], scalar=col(C_GFIN + c), in1=rstd[:, lo:hi],
                    op0=ALU.mult, op1=ALU.mult),
                    reads=R(X[c], lo, hi) + R(B_rstd, lo, hi) + [B_cst], writes=R(X[c], lo, hi))
        for tt in range(1, T // 128):
            s = tt % 2
            for q in range(4):
                bk = main_bank()
                fns = [(lambda h, bk=bk, j=j, q=q, tt=tt: h.transpose(
                    ps[:, bk, j * 128:(j + 1) * 128], xs[:, 4 * q + j, tt * 128:(tt + 1) * 128], ident[:]))
                    for j in range(4)]
                rd = [B_id]
                for j in range(4):
                    rd += R(X[4 * q + j], tt * 128, (tt + 1) * 128)
                P.group("pe", fns, reads=rd, writes=[PS[bk]])
                if q % 2 == 0:
                    P.op("dve", lambda h, bk=bk, q=q, s=s: h.tensor_copy(out=xin[s][:, q * 512:(q + 1) * 512], in_=ps[:, bk, :]),
                         reads=[PS[bk]], writes=B_xin[s])
                else:
                    P.op("act", lambda h, bk=bk, q=q, s=s: h.activation(out=xin[s][:, q * 512:(q + 1) * 512], in_=ps[:, bk, :],
                                                                       func=AF.Copy),
                         reads=[PS[bk]], writes=B_xin[s])
            P.dma("sp", lambda h, tt=tt, s=s: h.dma_start(out=y_d[(tt - 1) * 128:tt * 128, :], in_=xin[s][:]),
                  stsem, reads=B_xin[s])
        P.wait_tok("sp", (stsem[0], stsem[1]))
        P.replay()
    return nc


def _percol(v):
    v = np.asarray(v, dtype=np.float32)
    lead = v.shape[:-1]
    nchunk = v.shape[-1] // 128
    v = v.reshape(lead + (nchunk, 128))
    v = np.moveaxis(v, -1, 0)
    return np.ascontiguousarray(v.reshape(128, -1))


def _build_consts(inputs, q):
    cst = np.zeros((128, NCST), dtype=np.float32)
    cst[:, C_GMIX:C_GMIX + 64] = _percol(inputs["norm_mix_g"])
    cst[:, C_GMLP:C_GMLP + 64] = _percol(inputs["norm_mlp_g"])
    cst[:, C_GFIN:C_GFIN + 16] = _percol(inputs["final_g"])
    caw = np.asarray(inputs["conv_a_w"], dtype=np.float32).reshape(2, 31, 8, 128)
    cst[:, C_CAW:C_CAW + 496] = np.transpose(caw, (3, 0, 2, 1)).reshape(128, 496)
    cst[:, C_CAB:C_CAB + 16] = _percol(inputs["conv_a_b"])
    cst[:, C_LNG:C_LNG + 16] = _percol(inputs["ln_a_g"])
    cst[:, C_LNB:C_LNB + 16] = _percol(inputs["ln_a_b"])
    cbw = np.asarray(inputs["conv_b_w"], dtype=np.float32).reshape(2, 3, 8, 128)
    cst[:, C_CBW:C_CBW + 48] = np.transpose(cbw, (3, 0, 2, 1)).reshape(128, 48)
    cst[:, C_PSC:C_PSC + 32] = _percol(inputs["pool_scale"])
    cst[:, C_MASK] = 0.0 if q == 0 else 1.0
    for g, win in enumerate(POOL_WINDOWS):
        for t in range(16):
            cnt = min(t + 1, win) if q == 0 else win
            cst[:, C_INVC + g * 16 + t] = 1.0 / cnt
    cst[:, C_EPS_RMS] = RMS_EPS
    cst[:, C_EPS_LN] = LN_EPS
    return cst


_NC_CACHE = {}


def kernel(**inputs):
    x = np.asarray(inputs["x"], dtype=np.float32)
    if "nc" not in _NC_CACHE:
        _NC_CACHE["nc"] = build_program()
    nc = _NC_CACHE["nc"]
    ident = np.eye(128, dtype=np.float32)
    w_in = np.ascontiguousarray(inputs["w_in_even"], dtype=np.float32)
    w_out = np.ascontiguousarray(inputs["w_out_even"], dtype=np.float32)
    pool_w = np.ascontiguousarray(inputs["pool_w"], dtype=np.float32)
    w1 = np.ascontiguousarray(inputs["mlp_w1"], dtype=np.float32)
    w2 = np.ascontiguousarray(inputs["mlp_w2"], dtype=np.float32)
    in_maps = []
    for core in range(8):
        b, q = core // 4, core % 4
        xsh = np.zeros((T, D), dtype=np.float32)
        lo = q * OWN - HALO
        if lo < 0:
            xsh[HALO:] = x[b, 0:OWN]
        else:
            xsh[:] = x[b, lo:lo + T]
        in_maps.append({"x": xsh, "cst": _build_consts(inputs, q), "ident": ident, "w_in": w_in, "w_out": w_out,
                        "pool_w": pool_w, "w1": w1, "w2": w2})
    res = run_bass_kernel_spmd(nc, in_maps, core_ids=list(range(8)))
    out = np.empty((2, SEQ, D), dtype=np.float32)
    for core in range(8):
        b, q = core // 4, core % 4
        out[b, q * OWN:(q + 1) * OWN] = res.results[core]["y"]
    return out
```

```python
from contextlib import ExitStack

import numpy as np
import concourse.bass as bass
import concourse.mybir as mybir
from concourse.bass_utils import run_bass_kernel_spmd

F32 = mybir.dt.float32
BF16 = mybir.dt.bfloat16
AF = mybir.ActivationFunctionType
ALU = mybir.AluOpType

D = 2048
KC = 16
SEQ = 4096
OWN = 1024
HALO = 128
T = OWN + HALO
NT = 384
NN = T // NT
WB = 256
NBUF = 3
DEPTH = 4
RMS_EPS = 1e-6
LN_EPS = 1e-5
POOL_WINDOWS = (2, 4, 8, 16)

C_GMIX = 0
C_GMLP = C_GMIX + 64
C_GFIN = C_GMLP + 64
C_CAW = C_GFIN + 16
C_CAB = C_CAW + 496
C_LNG = C_CAB + 16
C_LNB = C_LNG + 16
C_CBW = C_LNB + 16
C_PSC = C_CBW + 48
C_MASK = C_PSC + 32
C_INVC = C_MASK + 1
C_EPS_RMS = C_INVC + 64
C_EPS_LN = C_EPS_RMS + 1
NCST = C_EPS_LN + 1


class Buf:
    __slots__ = ("name", "w", "r")

    def __init__(self, name):
        self.name = name
        self.w = None
        self.r = []


class Eng:
    def __init__(self, name, sem):
        self.name = name
        self.sem = sem
        self.count = 0
        self.seen = {}
        self.ops = []


class Prog:
    def __init__(self, nc, stack):
        self.nc = nc
        self.stack = stack
        self.eng = {}
        for n in ("pe", "dve", "act", "pool", "sp"):
            self.eng[n] = Eng(n, stack.enter_context(nc.semaphore("s_" + n)))

    def new_sem(self, name):
        return [self.stack.enter_context(self.nc.semaphore(name)), 0]

    def _waits(self, e, reads, writes, extra=()):
        deps = list(extra)
        for b in reads:
            if b.w is not None:
                deps.append(b.w)
        for b in writes:
            if b.w is not None:
                deps.append(b.w)
            deps.extend(b.r)
        need = {}
        for (sem, val) in deps:
            if e.name == "pe" and sem is e.sem:
                continue
            k = id(sem)
            if e.seen.get(k, 0) >= val:
                continue
            if k not in need or need[k][1] < val:
                need[k] = (sem, val)
        out = []
        for k, (sem, val) in need.items():
            e.seen[k] = val
            out.append((sem, val))
        return out

    @staticmethod
    def _commit(tok, reads, writes):
        for b in writes:
            b.w = tok
            b.r = []
        for b in reads:
            b.r.append(tok)

    def op(self, eng, fn, reads=(), writes=()):
        e = self.eng[eng]
        waits = self._waits(e, reads, writes)
        e.count += 1
        tok = (e.sem, e.count)
        e.ops.append((waits, fn, (e.sem, 1)))
        self._commit(tok, reads, writes)
        return tok

    def group(self, eng, fns, reads=(), writes=()):
        e = self.eng[eng]
        waits = self._waits(e, reads, writes)
        e.count += 1
        tok = (e.sem, e.count)
        last = len(fns) - 1
        for i, fn in enumerate(fns):
            e.ops.append((waits if i == 0 else [], fn, (e.sem, 1) if i == last else None))
        self._commit(tok, reads, writes)
        return tok

    def dma(self, eng, fn, dsem, reads=(), writes=()):
        e = self.eng[eng]
        waits = self._waits(e, reads, writes)
        dsem[1] += 16
        tok = (dsem[0], dsem[1])
        e.ops.append((waits, fn, (dsem[0], 16)))
        self._commit(tok, reads, writes)
        return tok

    def wait_tok(self, eng, tok):
        e = self.eng[eng]
        waits = self._waits(e, (), (), extra=[tok])
        if waits:
            e.ops.append((waits, None, None))

    def replay(self):
        handles = {"pe": "tensor", "dve": "vector", "act": "scalar", "pool": "gpsimd", "sp": "sync"}
        with self.nc.Block() as block:
            for n, attr in handles.items():
                e = self.eng[n]

                def body(h, e=e):
                    for waits, fn, inc in e.ops:
                        for (sem, val) in waits:
                            h.wait_ge(sem, val)
                        if fn is not None:
                            ins = fn(h)
                            if inc is not None:
                                ins.then_inc(inc[0], inc[1])

                getattr(block, attr)(body)


def build_program():
    nc = bass.Bass("TRN2", target_bir_lowering=False)
    x_d = nc.dram_tensor("x", [T, D], F32, kind="ExternalInput").ap()
    cst_d = nc.dram_tensor("cst", [128, NCST], F32, kind="ExternalInput").ap()
    id_d = nc.dram_tensor("ident", [128, 128], F32, kind="ExternalInput").ap()
    win_d = nc.dram_tensor("w_in", [2, D, 5120], F32, kind="ExternalInput").ap()
    wout_d = nc.dram_tensor("w_out", [2, D, D], F32, kind="ExternalInput").ap()
    pw_d = nc.dram_tensor("pool_w", [2, 4, 512, 512], F32, kind="ExternalInput").ap()
    w1_d = nc.dram_tensor("w1", [DEPTH, D, 4 * D], F32, kind="ExternalInput").ap()
    w2_d = nc.dram_tensor("w2", [DEPTH, 4 * D, D], F32, kind="ExternalInput").ap()
    y_d = nc.dram_tensor("y", [OWN, D], F32, kind="ExternalOutput").ap()

    with ExitStack() as st:
        P = Prog(nc, st)

        def sb(name, shape, dt):
            return st.enter_context(nc.sbuf_tensor(name, shape, dt))

        xs = sb("xs", [128, KC, T], F32)
        hb = sb("hb", [128, KC, T], BF16)
        cb = sb("cb", [128, KC, T], BF16)
        wblk = [sb(f"wblk{i}", [128, KC * WB], BF16) for i in range(NBUF)]
        cst = sb("cst_sb", [128, NCST], F32)
        ident = sb("ident_sb", [128, 128], F32)
        ones = sb("ones", [128, 128], BF16)
        TP = 32 + T
        scr = sb("scr", [128, 3 * TP + 3 * T], F32)
        tA = scr[:, 0:TP]
        tB = scr[:, TP:2 * TP]
        tC = scr[:, 2 * TP:3 * TP]
        rstd = scr[:, 3 * TP:3 * TP + T]
        mu = scr[:, 3 * TP + T:3 * TP + 2 * T]
        tD = scr[:, 3 * TP + 2 * T:3 * TP + 3 * T]
        xin = [scr[:, 0:D], scr[:, 3 * TP + T:3 * TP + T + D]]
        sq = [sb(f"sq{i}", [128, NT], BF16) for i in range(3)]
        rtall = sb("rtall", [128, 3 * NT], F32)
        rt = [rtall[:, i * NT:(i + 1) * NT] for i in range(3)]
        tP = rtall[:, 0:T]
        ps = st.enter_context(nc.psum_tensor("ps", [128, 8, 512], F32))

        G = 16
        NG = T // G

        def blist(name):
            return [Buf(f"{name}_{g}") for g in range(NG)]

        def R(bl, lo, hi):
            return bl[lo // G:(hi + G - 1) // G]

        X = [blist(f"x{c}") for c in range(KC)]
        H = [blist(f"h{c}") for c in range(KC)]
        C = [blist(f"c{c}") for c in range(KC)]
        Wb = [Buf(f"w{i}") for i in range(NBUF)]
        PS = [Buf(f"ps{i}") for i in range(8)]
        B_cst, B_id, B_ones = Buf("cst"), Buf("ident"), Buf("ones")
        B_tA, B_tB = Buf("tA"), Buf("tB")
        B_tC = blist("tC")
        B_rstd = blist("rstd")
        B_mu = blist("mu")
        B_tD = blist("tD")
        B_xin = [[B_tA, B_tB], B_mu + B_tD]
        B_sq = [Buf(f"sq{i}") for i in range(3)]
        B_rt = [Buf(f"rt{i}") for i in range(3)]

        ld_cst = P.new_sem("ld_cst")
        ld_id = P.new_sem("ld_id")
        xld = [P.new_sem("xld0"), P.new_sem("xld1")]
        wsem = [P.new_sem(f"wl{i}") for i in range(NBUF)]
        stsem = P.new_sem("st")

        cnt = {"main": 0, "aux": 0, "w": 0, "sq": 0, "rt": 0}

        def main_bank():
            b = cnt["main"] % 6
            cnt["main"] += 1
            return b

        def aux_bank():
            b = 6 + cnt["aux"] % 2
            cnt["aux"] += 1
            return b

        def split(t0):
            L = T - t0
            base = (L // 48) * 16
            extra = (L - 3 * base) // 16
            tiles = []
            lo = t0
            for i in range(3):
                w = base + (16 if i < extra else 0)
                tiles.append((lo, lo + w))
                lo += w
            assert lo == T and all(hi - lo_ <= NT for lo_, hi in tiles)
            return tiles

        def col(i):
            return cst[:, i:i + 1]

        P.dma("sp", lambda h: h.dma_start(out=cst[:], in_=cst_d), ld_cst, writes=[B_cst])
        P.dma("sp", lambda h: h.dma_start(out=ident[:], in_=id_d), ld_id, writes=[B_id])
        P.op("dve", lambda h: h.memset(ones[:], 1.0), writes=[B_ones])
        P.op("dve", lambda h: h.memset(tA[:], 0.0), writes=[B_tA])
        P.op("dve", lambda h: h.memset(tB[:], 0.0), writes=[B_tB])
        P.op("dve", lambda h: h.memset(tC[:], 0.0), writes=B_tC)

        for tt in range(T // 128):
            s = tt % 2
            P.dma("sp", lambda h, tt=tt, s=s: h.dma_start(out=xin[s][:], in_=x_d[tt * 128:(tt + 1) * 128, :]),
                  xld[s], writes=B_xin[s])
            for q in range(4):
                bk = main_bank()
                fns = [(lambda h, bk=bk, j=j, q=q, s=s: h.transpose(
                    ps[:, bk, j * 128:(j + 1) * 128], xin[s][:, (4 * q + j) * 128:(4 * q + j + 1) * 128], ident[:]))
                    for j in range(4)]
                P.group("pe", fns, reads=B_xin[s] + [B_id], writes=[PS[bk]])
                wr = []
                for j in range(4):
                    wr += R(X[4 * q + j], tt * 128, (tt + 1) * 128)
                if q % 2 == 0:
                    fn = lambda h, bk=bk, q=q, tt=tt: h.tensor_copy(
                        out=xs[:, 4 * q:4 * q + 4, tt * 128:(tt + 1) * 128],
                        in_=ps[:, bk, :].rearrange("p (j t) -> p j t", t=128))
                    P.op("dve", fn, reads=[PS[bk]], writes=wr)
                else:
                    fn = lambda h, bk=bk, q=q, tt=tt: h.activation(
                        out=xs[:, 4 * q:4 * q + 4, tt * 128:(tt + 1) * 128],
                        in_=ps[:, bk, :].rearrange("p (j t) -> p j t", t=128), func=AF.Copy)
                    P.op("act", fn, reads=[PS[bk]], writes=wr)

        def load_wblock(src_ap, nk, ncols):
            s = cnt["w"] % NBUF
            cnt["w"] += 1
            slot_gen[s] += 1
            assert ncols * 2 >= 512, "cast-DMA SBUF-side runs below 512 B faulted the device under trace (R3-R5)"
            view = wblk[s][:, 0:nk * ncols].rearrange("p (k m) -> p k m", m=ncols)
            P.dma("pool", lambda h: h.dma_start(out=view, in_=src_ap.rearrange("(k p) m -> p k m", p=128)),
                  wsem[s], writes=[Wb[s]])
            return (s, slot_gen[s]), view

        def mm_item(bk, s, view, mi, nk, src, chunks, lo, hi, SRC):
            fns = [(lambda h, k=k: h.matmul(ps[:, bk, 0:hi - lo], lhsT=view[:, k, mi * 128:(mi + 1) * 128],
                                            rhs=src[:, chunks[k], lo:hi], start=(k == 0), stop=(k == nk - 1)))
                   for k in range(nk)]
            s, gen = s
            assert slot_gen[s] == gen, f"weight slot {s} was reloaded (gen {slot_gen[s]}) while a gen-{gen} reader is still being emitted"
            rb = [Wb[s]]
            for k in range(nk):
                rb += R(SRC[chunks[k]], lo, hi)
            P.group("pe", fns, reads=rb, writes=[PS[bk]])

        ALLK = list(range(KC))
        slot_gen = [0] * NBUF

        def rms_rstd(tiles):
            for (lo, hi) in tiles:
                w = hi - lo
                bk = aux_bank()
                for c in range(KC):
                    i = cnt["sq"] % 3
                    cnt["sq"] += 1
                    P.op("act", lambda h, c=c, i=i, lo=lo, hi=hi, w=w: h.activation(out=sq[i][:, 0:w], in_=xs[:, c, lo:hi], func=AF.Square),
                         reads=R(X[c], lo, hi), writes=[B_sq[i]])
                    P.op("pe", lambda h, c=c, i=i, bk=bk, w=w: h.matmul(ps[:, bk, 0:w], lhsT=ones[:], rhs=sq[i][:, 0:w],
                                                                     start=(c == 0), stop=(c == KC - 1)),
                         reads=[B_sq[i], B_ones], writes=[PS[bk]])
                P.op("act", lambda h, lo=lo, hi=hi, w=w, bk=bk: h.activation(out=tD[:, lo:hi], in_=ps[:, bk, 0:w], func=AF.Sqrt,
                                                                        bias=col(C_EPS_RMS), scale=1.0 / D),
                     reads=[PS[bk], B_cst], writes=R(B_tD, lo, hi))
                P.op("dve", lambda h, lo=lo, hi=hi: h.reciprocal(out=rstd[:, lo:hi], in_=tD[:, lo:hi]),
                     reads=R(B_tD, lo, hi), writes=R(B_rstd, lo, hi))

        def rmsnorm_to_hb(gbase, tiles):
            rms_rstd(tiles)
            for (lo, hi) in tiles:
                for c in range(KC):
                    P.op("dve", lambda h, c=c, lo=lo, hi=hi: h.scalar_tensor_tensor(
                        out=hb[:, c, lo:hi], in0=xs[:, c, lo:hi], scalar=col(gbase + c), in1=rstd[:, lo:hi],
                        op0=ALU.mult, op1=ALU.mult),
                        reads=R(X[c], lo, hi) + R(B_rstd, lo, hi) + [B_cst], writes=R(H[c], lo, hi))

        def halo_mask():
            xb = []
            for c in range(KC):
                xb += R(X[c], 0, HALO)
            P.op("dve", lambda h: h.tensor_scalar(out=xs[:, :, 0:HALO], in0=xs[:, :, 0:HALO], scalar1=col(C_MASK),
                                                  scalar2=None, op0=ALU.mult),
                 reads=[B_cst] + xb, writes=xb)


        def even_mixer(layer, utiles, otiles):
            i = layer // 2
            u0 = utiles[0][0]
            o0 = otiles[0][0]
            assert o0 - u0 >= 30
            rmsnorm_to_hb(C_GMIX + layer * KC, utiles)
            wv = {}

            def A_mm(j):
                blk, mi = divmod(j, 2)
                if mi == 0:
                    wv["g"] = load_wblock(win_d[i][:, 1024 + blk * WB:1024 + (blk + 1) * WB], KC, WB)
                    wv["v"] = load_wblock(win_d[i][:, blk * WB:(blk + 1) * WB], KC, WB)
                sg_, vg = wv["g"]
                sv_, vv = wv["v"]
                for (lo, hi) in utiles:
                    bk = main_bank()
                    mm_item(bk, sg_, vg, mi, KC, hb, ALLK, lo, hi, H)
                    P.op("act", lambda h, bk=bk, lo=lo, hi=hi: h.activation(out=tC[:, lo:hi], in_=ps[:, bk, 0:hi - lo], func=AF.Sigmoid),
                         reads=[PS[bk]], writes=R(B_tC, lo, hi))
                for (lo, hi) in utiles:
                    bk = main_bank()
                    mm_item(bk, sv_, vv, mi, KC, hb, ALLK, lo, hi, H)
                    P.op("dve", lambda h, bk=bk, lo=lo, hi=hi: h.tensor_tensor(
                        out=tA[:, 30 + lo:30 + hi], in0=ps[:, bk, 0:hi - lo], in1=tC[:, lo:hi], op=ALU.mult),
                        reads=[PS[bk]] + R(B_tC, lo, hi), writes=[B_tA])

            def A_conv(j):
                wbase = C_CAW + (i * 8 + j) * 31
                P.op("dve", lambda h: h.tensor_scalar(out=tB[:, o0:T], in0=tA[:, o0:T], scalar1=col(wbase),
                                                      scalar2=col(C_CAB + i * 8 + j), op0=ALU.mult, op1=ALU.add),
                     reads=[B_tA, B_cst], writes=[B_tB])
                for k in range(1, 30):
                    P.op("dve", lambda h, k=k: h.scalar_tensor_tensor(
                        out=tB[:, o0:T], in0=tA[:, o0 + k:T + k], scalar=col(wbase + k), in1=tB[:, o0:T],
                        op0=ALU.mult, op1=ALU.add),
                        reads=[B_tA, B_tB, B_cst], writes=[B_tB])
                P.op("dve", lambda h: h.scalar_tensor_tensor(
                    out=cb[:, j, o0:T], in0=tA[:, o0 + 30:T + 30], scalar=col(wbase + 30), in1=tB[:, o0:T],
                    op0=ALU.mult, op1=ALU.add),
                    reads=[B_tA, B_tB, B_cst], writes=R(C[j], o0, T))

            def B_mm(j):
                blk, mi = divmod(j, 2)
                if mi == 0:
                    wv["x"] = load_wblock(win_d[i][:, 2048 + blk * WB:2048 + (blk + 1) * WB], KC, WB)
                    wv["c"] = load_wblock(win_d[i][:, 3072 + blk * WB:3072 + (blk + 1) * WB], KC, WB)
                    wv["b"] = load_wblock(win_d[i][:, 4096 + blk * WB:4096 + (blk + 1) * WB], KC, WB)
                sx_, vx = wv["x"]
                sc_, vc = wv["c"]
                sb_, vb = wv["b"]
                for (lo, hi) in utiles:
                    bk = main_bank()
                    mm_item(bk, sx_, vx, mi, KC, hb, ALLK, lo, hi, H)
                    P.op("act", lambda h, bk=bk, lo=lo, hi=hi: h.activation(out=mu[:, lo:hi], in_=ps[:, bk, 0:hi - lo], func=AF.Copy),
                         reads=[PS[bk]], writes=R(B_mu, lo, hi))
                for (lo, hi) in utiles:
                    bk = main_bank()
                    mm_item(bk, sc_, vc, mi, KC, hb, ALLK, lo, hi, H)
                    P.op("act", lambda h, bk=bk, lo=lo, hi=hi: h.activation(out=tD[:, lo:hi], in_=ps[:, bk, 0:hi - lo], func=AF.Copy),
                         reads=[PS[bk]], writes=R(B_tD, lo, hi))
                P.op("pool", lambda h: h.tensor_tensor(out=tD[:, u0:T], in0=tD[:, u0:T], in1=mu[:, u0:T], op=ALU.mult),
                     reads=R(B_tD, u0, T) + R(B_mu, u0, T), writes=R(B_tD, u0, T))
                wbase = C_CBW + (i * 8 + j) * 3
                P.op("act", lambda h: h.activation(out=rstd[:, o0:T], in_=tD[:, o0 - 2:T - 2], func=AF.Identity, scale=col(wbase)),
                     reads=R(B_tD, o0 - 2, T) + [B_cst], writes=R(B_rstd, o0, T))
                for k in range(1, 3):
                    P.op("act", lambda h, k=k: h.activation(out=tP[:, o0:T], in_=tD[:, o0 - 2 + k:T - 2 + k], func=AF.Identity,
                                                          scale=col(wbase + k)),
                         reads=R(B_tD, o0 - 2, T) + [B_cst], writes=B_rt)
                    P.op("pool", lambda h: h.tensor_tensor(out=rstd[:, o0:T], in0=rstd[:, o0:T], in1=tP[:, o0:T], op=ALU.add),
                         reads=R(B_rstd, o0, T) + B_rt, writes=R(B_rstd, o0, T))
                for (lo, hi) in otiles:
                    bk = main_bank()
                    mm_item(bk, sb_, vb, mi, KC, hb, ALLK, lo, hi, H)
                    P.op("act", lambda h, bk=bk, lo=lo, hi=hi: h.activation(out=mu[:, lo:hi], in_=ps[:, bk, 0:hi - lo], func=AF.Copy),
                         reads=[PS[bk]], writes=R(B_mu, lo, hi))
                P.op("pool", lambda h: h.tensor_tensor(out=cb[:, 8 + j, o0:T], in0=mu[:, o0:T], in1=rstd[:, o0:T], op=ALU.mult),
                     reads=R(B_mu, o0, T) + R(B_rstd, o0, T), writes=R(C[8 + j], o0, T))

            for r in range(4):
                A_mm(2 * r)
                A_conv(2 * r)
                A_mm(2 * r + 1)
                if r > 0:
                    B_mm(2 * r - 2)
                    B_mm(2 * r - 1)
                A_conv(2 * r + 1)
            B_mm(6)
            B_mm(7)
            for (lo, hi) in otiles:
                w = hi - lo
                b1 = aux_bank()
                b2 = aux_bank()
                for j in range(8):
                    q = cnt["sq"] % 3
                    cnt["sq"] += 1
                    P.op("act", lambda h, j=j, q=q, lo=lo, hi=hi, w=w: h.activation(out=sq[q][:, 0:w], in_=cb[:, j, lo:hi], func=AF.Square),
                         reads=R(C[j], lo, hi), writes=[B_sq[q]])
                    P.op("pe", lambda h, j=j, b1=b1, lo=lo, hi=hi, w=w: h.matmul(ps[:, b1, 0:w], lhsT=ones[:], rhs=cb[:, j, lo:hi],
                                                                               start=(j == 0), stop=(j == 7)),
                         reads=R(C[j], lo, hi) + [B_ones], writes=[PS[b1]])
                    P.op("pe", lambda h, j=j, q=q, b2=b2, w=w: h.matmul(ps[:, b2, 0:w], lhsT=ones[:], rhs=sq[q][:, 0:w],
                                                                     start=(j == 0), stop=(j == 7)),
                         reads=[B_sq[q], B_ones], writes=[PS[b2]])
                P.op("dve", lambda h, b1=b1, lo=lo, hi=hi, w=w: h.tensor_scalar(out=tA[:, lo:hi], in0=ps[:, b1, 0:w], scalar1=1.0 / 1024,
                                                                           scalar2=None, op0=ALU.mult),
                     reads=[PS[b1]], writes=[B_tA])
                P.op("dve", lambda h, lo=lo, hi=hi: h.tensor_tensor(out=tP[:, lo:hi], in0=tA[:, lo:hi], in1=tA[:, lo:hi], op=ALU.mult),
                     reads=[B_tA], writes=B_rt)
                P.op("dve", lambda h, b2=b2, lo=lo, hi=hi, w=w: h.scalar_tensor_tensor(
                    out=tP[:, lo:hi], in0=ps[:, b2, 0:w], scalar=1.0 / 1024, in1=tP[:, lo:hi],
                    op0=ALU.mult, op1=ALU.subtract),
                    reads=[PS[b2]] + B_rt, writes=B_rt)
                P.op("act", lambda h, lo=lo, hi=hi: h.activation(out=tP[:, lo:hi], in_=tP[:, lo:hi], func=AF.Sqrt,
                                                               bias=col(C_EPS_LN), scale=1.0),
                     reads=B_rt + [B_cst], writes=B_rt)
                P.op("dve", lambda h, lo=lo, hi=hi: h.reciprocal(out=tC[:, lo:hi], in_=tP[:, lo:hi]),
                     reads=B_rt, writes=R(B_tC, lo, hi))
            for j in range(8):
                P.op("dve", lambda h, j=j: h.tensor_tensor(out=tB[:, o0:T], in0=cb[:, j, o0:T], in1=tA[:, o0:T], op=ALU.subtract),
                     reads=R(C[j], o0, T) + [B_tA], writes=[B_tB])
                P.op("dve", lambda h: h.tensor_tensor(out=tB[:, o0:T], in0=tB[:, o0:T], in1=tC[:, o0:T], op=ALU.mult),
                     reads=[B_tB] + R(B_tC, o0, T), writes=[B_tB])
                P.op("act", lambda h, j=j: h.activation(out=cb[:, j, o0:T], in_=tB[:, o0:T], func=AF.Silu,
                                                      bias=col(C_LNB + i * 8 + j), scale=col(C_LNG + i * 8 + j)),
                     reads=[B_tB, B_cst], writes=R(C[j], o0, T))
            for blk in range(D // WB):
                so_, vo = load_wblock(wout_d[i][:, blk * WB:(blk + 1) * WB], KC, WB)
                for mi in range(2):
                    m = 2 * blk + mi
                    for (lo, hi) in otiles:
                        bk = main_bank()
                        mm_item(bk, so_, vo, mi, KC, cb, ALLK, lo, hi, C)
                        P.op("dve", lambda h, bk=bk, lo=lo, hi=hi, m=m: h.tensor_tensor(
                            out=xs[:, m, lo:hi], in0=ps[:, bk, 0:hi - lo], in1=xs[:, m, lo:hi], op=ALU.add),
                            reads=[PS[bk]] + R(X[m], lo, hi), writes=R(X[m], lo, hi))
            halo_mask()

        def odd_mixer(layer, stiles, tiles):
            i = layer // 2
            e0 = stiles[0][0]
            assert tiles[0][0] - e0 >= 15
            rms_rstd(stiles)
            PADW = 16
            for c in range(KC):
                g = c // 4
                win = POOL_WINDOWS[g]
                P.op("dve", lambda h, c=c: h.scalar_tensor_tensor(
                    out=tA[:, PADW + e0:PADW + T], in0=xs[:, c, e0:T], scalar=col(C_GMIX + layer * KC + c), in1=rstd[:, e0:T],
                    op0=ALU.mult, op1=ALU.mult),
                    reads=R(X[c], e0, T) + R(B_rstd, e0, T) + [B_cst], writes=[B_tA])
                src, sbuf_ = tA, [B_tA]
                dsts = [(tB, [B_tB]), (tC, B_tC)]
                sh = 1
                di = 0
                while sh < win:
                    dst, dbuf = dsts[di % 2]
                    di += 1
                    P.op("dve", lambda h, src=src, dst=dst, sh=sh: h.tensor_tensor(
                        out=dst[:, PADW + e0:PADW + T], in0=src[:, PADW + e0:PADW + T], in1=src[:, PADW + e0 - sh:PADW + T - sh], op=ALU.add),
                        reads=sbuf_, writes=dbuf)
                    src, sbuf_ = dst, dbuf
                    sh *= 2
                P.op("dve", lambda h, c=c, src=src, win=win: h.scalar_tensor_tensor(
                    out=hb[:, c, e0:T], in0=src[:, PADW + e0:PADW + T], scalar=1.0 / win, in1=tA[:, PADW + e0:PADW + T],
                    op0=ALU.mult, op1=ALU.subtract),
                    reads=sbuf_ + [B_tA], writes=R(H[c], e0, T))
                q = cnt["rt"] % 3
                cnt["rt"] += 1
                f0 = PADW + HALO
                P.op("dve", lambda h, src=src, g=g, q=q, f0=f0: h.tensor_tensor(
                    out=rt[q][:, 0:16], in0=src[:, f0:f0 + 16], in1=cst[:, C_INVC + g * 16:C_INVC + (g + 1) * 16], op=ALU.mult),
                    reads=sbuf_ + [B_cst], writes=[B_rt[q]])
                P.op("dve", lambda h, c=c, q=q, f0=f0: h.tensor_tensor(
                    out=hb[:, c, HALO:HALO + 16], in0=rt[q][:, 0:16], in1=tA[:, f0:f0 + 16], op=ALU.subtract),
                    reads=[B_rt[q], B_tA], writes=R(H[c], HALO, HALO + 16))
            for g in range(4):
                s_, v = load_wblock(pw_d[i][g], 4, 512)
                for mi in range(4):
                    m = 4 * g + mi
                    for (lo, hi) in tiles:
                        bk = main_bank()
                        mm_item(bk, s_, v, mi, 4, hb, [4 * g + k for k in range(4)], lo, hi, H)
                        P.op("dve", lambda h, bk=bk, lo=lo, hi=hi, m=m: h.scalar_tensor_tensor(
                            out=xs[:, m, lo:hi], in0=ps[:, bk, 0:hi - lo], scalar=col(C_PSC + i * KC + m), in1=xs[:, m, lo:hi],
                            op0=ALU.mult, op1=ALU.add),
                            reads=[PS[bk], B_cst] + R(X[m], lo, hi), writes=R(X[m], lo, hi))
            halo_mask()

        def mlp(layer, tiles):
            rmsnorm_to_hb(C_GMLP + layer * KC, tiles)
            for grp in range(4):
                for blk in range(8):
                    c0 = grp * D + blk * WB
                    s_, v = load_wblock(w1_d[layer][:, c0:c0 + WB], KC, WB)
                    for mi in range(2):
                        f = 2 * blk + mi
                        for (lo, hi) in tiles:
                            w = hi - lo
                            bk = main_bank()
                            mm_item(bk, s_, v, mi, KC, hb, ALLK, lo, hi, H)
                            q = cnt["rt"] % 3
                            cnt["rt"] += 1
                            P.op("act", lambda h, bk=bk, q=q, w=w: h.activation(out=rt[q][:, 0:w], in_=ps[:, bk, 0:w], func=AF.Relu),
                                 reads=[PS[bk]], writes=[B_rt[q]])
                            P.op("dve", lambda h, q=q, f=f, lo=lo, hi=hi, w=w: h.tensor_tensor(
                                out=cb[:, f, lo:hi], in0=rt[q][:, 0:w], in1=rt[q][:, 0:w], op=ALU.mult),
                                reads=[B_rt[q]], writes=R(C[f], lo, hi))
                for blk in range(8):
                    s_, v = load_wblock(w2_d[layer][grp * D:(grp + 1) * D, blk * WB:(blk + 1) * WB], KC, WB)
                    for mi in range(2):
                        m = 2 * blk + mi
                        for (lo, hi) in tiles:
                            bk = main_bank()
                            mm_item(bk, s_, v, mi, KC, cb, ALLK, lo, hi, C)
                            P.op("dve", lambda h, bk=bk, lo=lo, hi=hi, m=m: h.tensor_tensor(
                                out=xs[:, m, lo:hi], in0=ps[:, bk, 0:hi - lo], in1=xs[:, m, lo:hi], op=ALU.add),
                                reads=[PS[bk]] + R(X[m], lo, hi), writes=R(X[m], lo, hi))

        even_mixer(0, split(32), split(64))
        mlp(0, split(64))
        odd_mixer(1, split(64), split(80))
        mlp(1, split(80))
        even_mixer(2, split(80), split(112))
        mlp(2, split(112))
        odd_mixer(3, split(112), split(128))
        mlp(3, split(128))

        ftiles = split(HALO)
        rms_rstd(ftiles)
        for (lo, hi) in ftiles:
            for c in range(KC):
                P.op("dve", lambda h, c=c, lo=lo, hi=hi: h.scalar_tensor_tensor(
                    out=xs[:, c, lo:hi], in0=xs[:, c, lo:hi], scalar=col(C_GFIN + c), in1=rstd[:, lo:hi],
                    op0=ALU.mult, op1=ALU.mult),
                    reads=R(X[c], lo, hi) + R(B_rstd, lo, hi) + [B_cst], writes=R(X[c], lo, hi))
        for tt in range(1, T // 128):
            s = tt % 2
            for q in range(4):
                bk = main_bank()
                fns = [(lambda h, bk=bk, j=j, q=q, tt=tt: h.transpose(
                    ps[:, bk, j * 128:(j + 1) * 128], xs[:, 4 * q + j, tt * 128:(tt + 1) * 128], ident[:]))
                    for j in range(4)]
                rd = [B_id]
                for j in range(4):
                    rd += R(X[4 * q + j], tt * 128, (tt + 1) * 128)
                P.group("pe", fns, reads=rd, writes=[PS[bk]])
                if q % 2 == 0:
                    P.op("dve", lambda h, bk=bk, q=q, s=s: h.tensor_copy(out=xin[s][:, q * 512:(q + 1) * 512], in_=ps[:, bk, :]),
                         reads=[PS[bk]], writes=B_xin[s])
                else:
                    P.op("act", lambda h, bk=bk, q=q, s=s: h.activation(out=xin[s][:, q * 512:(q + 1) * 512], in_=ps[:, bk, :],
                                                                       func=AF.Copy),
                         reads=[PS[bk]], writes=B_xin[s])
            P.dma("sp", lambda h, tt=tt, s=s: h.dma_start(out=y_d[(tt - 1) * 128:tt * 128, :], in_=xin[s][:]),
                  stsem, reads=B_xin[s])
        P.wait_tok("sp", (stsem[0], stsem[1]))
        P.replay()
    return nc


def _percol(v):
    v = np.asarray(v, dtype=np.float32)
    lead = v.shape[:-1]
    nchunk = v.shape[-1] // 128
    v = v.reshape(lead + (nchunk, 128))
    v = np.moveaxis(v, -1, 0)
    return np.ascontiguousarray(v.reshape(128, -1))


def _build_consts(inputs, q):
    cst = np.zeros((128, NCST), dtype=np.float32)
    cst[:, C_GMIX:C_GMIX + 64] = _percol(inputs["norm_mix_g"])
    cst[:, C_GMLP:C_GMLP + 64] = _percol(inputs["norm_mlp_g"])
    cst[:, C_GFIN:C_GFIN + 16] = _percol(inputs["final_g"])
    caw = np.asarray(inputs["conv_a_w"], dtype=np.float32).reshape(2, 31, 8, 128)
    cst[:, C_CAW:C_CAW + 496] = np.transpose(caw, (3, 0, 2, 1)).reshape(128, 496)
    cst[:, C_CAB:C_CAB + 16] = _percol(inputs["conv_a_b"])
    cst[:, C_LNG:C_LNG + 16] = _percol(inputs["ln_a_g"])
    cst[:, C_LNB:C_LNB + 16] = _percol(inputs["ln_a_b"])
    cbw = np.asarray(inputs["conv_b_w"], dtype=np.float32).reshape(2, 3, 8, 128)
    cst[:, C_CBW:C_CBW + 48] = np.transpose(cbw, (3, 0, 2, 1)).reshape(128, 48)
    cst[:, C_PSC:C_PSC + 32] = _percol(inputs["pool_scale"])
    cst[:, C_MASK] = 0.0 if q == 0 else 1.0
    for g, win in enumerate(POOL_WINDOWS):
        for t in range(16):
            cnt = min(t + 1, win) if q == 0 else win
            cst[:, C_INVC + g * 16 + t] = 1.0 / cnt
    cst[:, C_EPS_RMS] = RMS_EPS
    cst[:, C_EPS_LN] = LN_EPS
    return cst


_NC_CACHE = {}


def kernel(**inputs):
    x = np.asarray(inputs["x"], dtype=np.float32)
    if "nc" not in _NC_CACHE:
        _NC_CACHE["nc"] = build_program()
    nc = _NC_CACHE["nc"]
    ident = np.eye(128, dtype=np.float32)
    w_in = np.ascontiguousarray(inputs["w_in_even"], dtype=np.float32)
    w_out = np.ascontiguousarray(inputs["w_out_even"], dtype=np.float32)
    pool_w = np.ascontiguousarray(inputs["pool_w"], dtype=np.float32)
    w1 = np.ascontiguousarray(inputs["mlp_w1"], dtype=np.float32)
    w2 = np.ascontiguousarray(inputs["mlp_w2"], dtype=np.float32)
    in_maps = []
    for core in range(8):
        b, q = core // 4, core % 4
        xsh = np.zeros((T, D), dtype=np.float32)
        lo = q * OWN - HALO
        if lo < 0:
            xsh[HALO:] = x[b, 0:OWN]
        else:
            xsh[:] = x[b, lo:lo + T]
        in_maps.append({"x": xsh, "cst": _build_consts(inputs, q), "ident": ident, "w_in": w_in, "w_out": w_out,
                        "pool_w": pool_w, "w1": w1, "w2": w2})
    res = run_bass_kernel_spmd(nc, in_maps, core_ids=list(range(8)))
    out = np.empty((2, SEQ, D), dtype=np.float32)
    for core in range(8):
        b, q = core // 4, core % 4
        out[b, q * OWN:(q + 1) * OWN] = res.results[core]["y"]
    return out
```
